# Optimizing a Trainium2 kernel written in Bass

```python
import math
import jax
import jax.numpy as jnp
from jax import lax

D_MODEL = 1024
BATCH = 8
SEQ = 8192
DEPTH = 2

GRID_W = 64
CTX_LEN = 256
GDN_HEADS = 8
GDN_DK = 128
GDN_DV = 128
GDN_QK = GDN_HEADS * GDN_DK
GDN_V = GDN_HEADS * GDN_DV
GDN_QKV = 2 * GDN_QK + GDN_V
SSM_INNER = 2 * D_MODEL
SSM_HEAD_DIM = 64
SSM_HEADS = SSM_INNER // SSM_HEAD_DIM
SSM_GROUPS = 8
SSM_HPG = SSM_HEADS // SSM_GROUPS
SSM_STATE = 128
SSM_GN = SSM_GROUPS * SSM_STATE
SSM_XBC = SSM_INNER + 2 * SSM_GN
CONV_K = 5
CHUNK = 64
D_FF = 4 * D_MODEL
DEEPNORM_ALPHA = (2 * DEPTH) ** 0.25
DEEPNORM_BETA = (8 * DEPTH) ** -0.25
LN_EPS = 1e-5
NORM_EPS = 1e-6
IN_SPLITS = (GDN_QKV, GDN_V, 2 * GDN_HEADS, 2 * GDN_HEADS, SSM_INNER, SSM_XBC, 2 * SSM_HEADS, D_MODEL, D_MODEL)
IN_DIM = sum(IN_SPLITS)

kernel_name = 'hybrid_gdn_ssd_deepnorm_dit'


def split_cols(t):
    pts, acc = [], 0
    for s in IN_SPLITS[:-1]:
        acc += s
        pts.append(acc)
    return jnp.split(t, pts, axis=-1)


def layer_norm(t, g, b):
    tf = t.astype(jnp.float32)
    mu = jnp.mean(tf, -1, keepdims=True)
    var = jnp.mean(jnp.square(tf - mu), -1, keepdims=True)
    return ((tf - mu) * lax.rsqrt(var + LN_EPS)).astype(t.dtype) * g + b


def l2norm(t):
    return t * lax.rsqrt(jnp.sum(t * t, -1, keepdims=True) + NORM_EPS)


def dwconv(t, w):
    pad = w.shape[0] // 2
    L = t.shape[-2]
    tp = jnp.pad(t, [(0, 0)] * (t.ndim - 2) + [(pad, pad), (0, 0)])
    out = tp[..., 0:L, :] * w[0]
    for j in range(1, w.shape[0]):
        out = out + tp[..., j:j + L, :] * w[j]
    return out


def chunk_front(t, axis):
    n = t.shape[axis] // CHUNK
    t = t.reshape(t.shape[:axis] + (n, CHUNK) + t.shape[axis + 1:])
    return jnp.moveaxis(t, axis, 0)


def unchunk(t, axis):
    t = jnp.moveaxis(t, 0, axis)
    return t.reshape(t.shape[:axis] + (-1,) + t.shape[axis + 2:])


def mlp(h, w1, b1, w2, b2):
    return jnp.square(jax.nn.relu(h @ w1 + b1)) @ w2 + b2


def gdn_prep(qkv, a_raw, b_raw, A_log, dt_bias):
    Bsz, L, _ = qkv.shape
    qkv = qkv.astype(jnp.float32)

    def heads(t, d):
        return t.reshape(Bsz, L, GDN_HEADS, d).transpose(0, 2, 1, 3)

    q = l2norm(heads(qkv[..., :GDN_QK], GDN_DK)) * GDN_DK ** -0.5
    k = l2norm(heads(qkv[..., GDN_QK:2 * GDN_QK], GDN_DK))
    v = heads(qkv[..., 2 * GDN_QK:], GDN_DV)
    a = a_raw.astype(jnp.float32).reshape(Bsz, L, 2, GDN_HEADS)
    bt = b_raw.astype(jnp.float32).reshape(Bsz, L, 2, GDN_HEADS)
    g = (-jnp.exp(A_log.astype(jnp.float32)) * jax.nn.softplus(a + dt_bias.astype(jnp.float32))).transpose(2, 0, 3, 1)
    beta = jax.nn.sigmoid(bt).transpose(2, 0, 3, 1)
    return q, k, v, g, beta


def gdn_chunk_scan(q, k, v, g, beta, S0, with_out):
    strict = jnp.tril(jnp.ones((CHUNK, CHUNK), bool), -1)
    incl = jnp.tril(jnp.ones((CHUNK, CHUNK), bool))
    eye = jnp.eye(CHUNK, dtype=jnp.float32)
    xs = (chunk_front(q, 2), chunk_front(k, 2), chunk_front(v, 2), chunk_front(g, 2), chunk_front(beta, 2))

    def step(S, inp):
        qc, kc, vc, gc, bc = inp
        gcum = jnp.cumsum(gc, axis=-1)
        glast = gcum[..., -1:]
        diff = gcum[..., :, None] - gcum[..., None, :]
        kk = jnp.einsum('bhid,bhjd->bhij', kc, kc)
        a_mat = jnp.where(strict, bc[..., :, None] * kk * jnp.exp(jnp.where(strict, diff, 0.0)), 0.0)
        rhs = jnp.concatenate([(bc * jnp.exp(gcum))[..., None] * kc, bc[..., None] * vc], axis=-1)
        wu = lax.linalg.triangular_solve(eye + a_mat, rhs, left_side=True, lower=True, unit_diagonal=True)
        w_blk, u_blk = wu[..., :GDN_DK], wu[..., GDN_DK:]
        v_new = u_blk - jnp.einsum('bhck,bhkv->bhcv', w_blk, S)
        S_next = jnp.exp(glast)[..., None] * S + jnp.einsum('bhck,bhcv->bhkv', kc * jnp.exp(glast - gcum)[..., None], v_new)
        if not with_out:
            return S_next, None
        qk = jnp.where(incl, jnp.einsum('bhid,bhjd->bhij', qc, kc) * jnp.exp(jnp.where(incl, diff, 0.0)), 0.0)
        o = jnp.einsum('bhck,bhkv->bhcv', qc * jnp.exp(gcum)[..., None], S) + jnp.einsum('bhij,bhjv->bhiv', qk, v_new)
        return S_next, o

    S_final, o = lax.scan(step, S0, xs)
    return S_final, (unchunk(o, 2) if with_out else None)


def gdn_bidir(lat, ctx, with_ctx_out):
    q, k, v, g, beta = lat
    qc, kc, vc, gc, bc = ctx
    S0 = jnp.zeros((q.shape[0], GDN_HEADS, GDN_DK, GDN_DV), jnp.float32)
    f = lambda t: jnp.flip(t, axis=2)
    s_f, oc_f = gdn_chunk_scan(qc, kc, vc, gc[0], bc[0], S0, with_ctx_out)
    s_b, oc_b = gdn_chunk_scan(f(qc), f(kc), f(vc), f(gc[1]), f(bc[1]), S0, with_ctx_out)
    _, o_f = gdn_chunk_scan(q, k, v, g[0], beta[0], s_f, True)
    _, o_b = gdn_chunk_scan(f(q), f(k), f(v), f(g[1]), f(beta[1]), s_b, True)
    o = o_f + f(o_b)
    o_c = (oc_f + f(oc_b)) if with_ctx_out else None
    return o, o_c


def gdn_gated_norm(o, gate, w):
    o = jnp.swapaxes(o, 1, 2)
    o = o * lax.rsqrt(jnp.mean(o * o, -1, keepdims=True) + NORM_EPS)
    Bsz, L = o.shape[:2]
    return (o.astype(gate.dtype) * w * jax.nn.silu(gate.reshape(Bsz, L, GDN_HEADS, GDN_DV))).reshape(Bsz, L, GDN_V)


def ssm_prep(xbc, dt_raw, A_log, dt_bias):
    Bsz, L, _ = xbc.shape
    xbc = xbc.astype(jnp.float32)
    xs = xbc[..., :SSM_INNER].reshape(Bsz, L, SSM_GROUPS, SSM_HPG, SSM_HEAD_DIM)
    Bm = xbc[..., SSM_INNER:SSM_INNER + SSM_GN].reshape(Bsz, L, SSM_GROUPS, SSM_STATE)
    Cm = xbc[..., SSM_INNER + SSM_GN:].reshape(Bsz, L, SSM_GROUPS, SSM_STATE)
    dt = jax.nn.softplus(dt_raw.astype(jnp.float32).reshape(Bsz, L, 2, SSM_HEADS) + dt_bias.astype(jnp.float32))
    dt = jnp.moveaxis(dt, 2, 0).reshape(2, Bsz, L, SSM_GROUPS, SSM_HPG)
    a_neg = -jnp.exp(A_log.astype(jnp.float32)).reshape(2, SSM_GROUPS, SSM_HPG)
    return xs, dt, a_neg, Bm, Cm


def ssd_chunk_scan(x, dt, a_neg, Bm, Cm, S0, with_out):
    incl = jnp.tril(jnp.ones((CHUNK, CHUNK), bool))
    xs = (chunk_front(x * dt[..., None], 1), chunk_front(dt * a_neg, 1), chunk_front(Bm, 1), chunk_front(Cm, 1))

    def step(S, inp):
        xdt, la, bc, cc = inp
        gcum = jnp.cumsum(la, axis=1)
        glast = gcum[:, -1]
        S_next = jnp.exp(glast)[..., None, None] * S + jnp.einsum('bcgn,bcgh,bcghp->bghnp', bc, jnp.exp(glast[:, None] - gcum), xdt)
        if not with_out:
            return S_next, None
        gt = jnp.moveaxis(gcum, 1, -1)
        diff = gt[..., :, None] - gt[..., None, :]
        decay = jnp.where(incl, jnp.exp(jnp.where(incl, diff, 0.0)), 0.0)
        cb = jnp.einsum('bign,bjgn->bgij', cc, bc)
        y = jnp.einsum('bgij,bghij,bjghp->bighp', cb, decay, xdt) + jnp.einsum('bign,bigh,bghnp->bighp', cc, jnp.exp(gcum), S)
        return S_next, y

    S_final, y = lax.scan(step, S0, xs)
    return S_final, (unchunk(y, 1) if with_out else None)


def ssd_bidir(lat, ctx, a_neg, D_skip, with_ctx_out):
    x, dt, Bm, Cm = lat
    xc, dtc, Bc, Cc = ctx
    S0 = jnp.zeros((x.shape[0], SSM_GROUPS, SSM_HPG, SSM_STATE, SSM_HEAD_DIM), jnp.float32)
    f = lambda t: jnp.flip(t, axis=1)
    d = D_skip.astype(jnp.float32).reshape(SSM_GROUPS, SSM_HPG, 1)
    s_f, yc_f = ssd_chunk_scan(xc, dtc[0], a_neg[0], Bc, Cc, S0, with_ctx_out)
    s_b, yc_b = ssd_chunk_scan(f(xc), f(dtc[1]), a_neg[1], f(Bc), f(Cc), S0, with_ctx_out)
    _, y_f = ssd_chunk_scan(x, dt[0], a_neg[0], Bm, Cm, s_f, True)
    _, y_b = ssd_chunk_scan(f(x), f(dt[1]), a_neg[1], f(Bm), f(Cm), s_b, True)
    y = y_f + f(y_b) + d * x
    y_c = (yc_f + f(yc_b) + d * xc) if with_ctx_out else None
    return y, y_c


def ssm_gated_norm(y, z, w):
    t = y * jax.nn.silu(z.astype(jnp.float32))
    shp = t.shape
    t = t.reshape(shp[:-1] + (SSM_GROUPS, -1))
    t = t * lax.rsqrt(jnp.mean(t * t, -1, keepdims=True) + NORM_EPS)
    return t.reshape(shp).astype(z.dtype) * w


def raster_to_cols_grid(t, rows):
    Bsz, L, C = t.shape
    return t.reshape(Bsz, rows, GRID_W, C).transpose(0, 2, 1, 3)


def cols_to_raster(t, rows):
    Bsz, L, C = t.shape
    return t.reshape(Bsz, GRID_W, rows, C).transpose(0, 2, 1, 3).reshape(Bsz, L, C)


def mixer(h, hc, rows, w_in, gdn_conv_w, gdn_A_log, gdn_dt_bias, gdn_norm_w,
          ssm_conv_w, ssm_conv_b, ssm_A_log, ssm_dt_bias, ssm_D, ssm_norm_w,
          w_proj_gdn, w_proj_ssm, w_out, with_ctx_out):
    Bsz, L, _ = h.shape
    qkv, gout, a_raw, b_raw, z, xbc, dt_raw, gate_a, gate_b = split_cols(h @ w_in)
    qkv_c, gout_c, a_c, b_c, z_c, xbc_c, dt_c, gate_ac, gate_bc = split_cols(hc @ w_in)

    qkv = jax.nn.silu(dwconv(qkv.reshape(Bsz, rows, GRID_W, GDN_QKV), gdn_conv_w).reshape(Bsz, L, GDN_QKV))
    qkv_c = jax.nn.silu(dwconv(qkv_c, gdn_conv_w))
    o, o_c = gdn_bidir(gdn_prep(qkv, a_raw, b_raw, gdn_A_log, gdn_dt_bias),
                       gdn_prep(qkv_c, a_c, b_c, gdn_A_log, gdn_dt_bias), with_ctx_out)
    y_a = gdn_gated_norm(o, gout, gdn_norm_w)

    xbc = jax.nn.silu(dwconv(raster_to_cols_grid(xbc, rows), ssm_conv_w) + ssm_conv_b).reshape(Bsz, L, SSM_XBC)
    dt_cols = raster_to_cols_grid(dt_raw, rows).reshape(Bsz, L, 2 * SSM_HEADS)
    xbc_c = jax.nn.silu(dwconv(xbc_c, ssm_conv_w) + ssm_conv_b)
    xs, dt, a_neg, Bm, Cm = ssm_prep(xbc, dt_cols, ssm_A_log, ssm_dt_bias)
    xs_c, dt_cc, _, Bc, Cc = ssm_prep(xbc_c, dt_c, ssm_A_log, ssm_dt_bias)
    y, y_c = ssd_bidir((xs, dt, Bm, Cm), (xs_c, dt_cc, Bc, Cc), a_neg, ssm_D, with_ctx_out)
    y_b = ssm_gated_norm(cols_to_raster(y.reshape(Bsz, L, SSM_INNER), rows), z, ssm_norm_w)

    out = (jax.nn.sigmoid(gate_a) * (y_a @ w_proj_gdn) + jax.nn.sigmoid(gate_b) * (y_b @ w_proj_ssm)) @ w_out
    if not with_ctx_out:
        return out, None
    Lc = hc.shape[1]
    y_ac = gdn_gated_norm(o_c, gout_c, gdn_norm_w)
    y_bc = ssm_gated_norm(y_c.reshape(Bsz, Lc, SSM_INNER), z_c, ssm_norm_w)
    out_c = (jax.nn.sigmoid(gate_ac) * (y_ac @ w_proj_gdn) + jax.nn.sigmoid(gate_bc) * (y_bc @ w_proj_ssm)) @ w_out
    return out, out_c


def inv_softplus_dt(k, shape):
    dt = jnp.exp(jax.random.uniform(k, shape, jnp.float32, math.log(1e-3), math.log(1e-1)))
    return dt + jnp.log(-jnp.expm1(-dt))


def setup_inputs(seed: int = 0) -> dict:
    key = jax.random.key(seed)
    ks = jax.random.split(key, 32)
    f32 = jnp.float32
    nrm = lambda k, shape, s: jax.random.normal(k, shape, f32) * s
    return {
        'x': nrm(ks[0], (BATCH, SEQ, D_MODEL), 1.0),
        'c': nrm(ks[1], (BATCH, D_MODEL), 1.0),
        'ctx': nrm(ks[2], (BATCH, CTX_LEN, D_MODEL), 1.0),
        'c_ctx': nrm(ks[3], (D_MODEL,), 1.0),
        'w_mod': nrm(ks[4], (DEPTH, D_MODEL, 6 * D_MODEL), 0.5 * D_MODEL ** -0.5),
        'b_mod': nrm(ks[5], (DEPTH, 6 * D_MODEL), 0.01),
        'w_in': nrm(ks[6], (DEPTH, D_MODEL, IN_DIM), D_MODEL ** -0.5),
        'gdn_conv_w': nrm(ks[7], (DEPTH, CONV_K, GDN_QKV), CONV_K ** -0.5),
        'gdn_A_log': jnp.log(jax.random.uniform(ks[8], (DEPTH, 2, GDN_HEADS), f32, 1.0, 16.0)),
        'gdn_dt_bias': inv_softplus_dt(ks[9], (DEPTH, 2, GDN_HEADS)),
        'gdn_norm_w': 1.0 + nrm(ks[10], (DEPTH, GDN_DV), 0.02),
        'ssm_conv_w': nrm(ks[11], (DEPTH, CONV_K, SSM_XBC), CONV_K ** -0.5),
        'ssm_conv_b': nrm(ks[12], (DEPTH, SSM_XBC), 0.01),
        'ssm_A_log': jnp.log(jax.random.uniform(ks[13], (DEPTH, 2, SSM_HEADS), f32, 1.0, 16.0)),
        'ssm_dt_bias': inv_softplus_dt(ks[14], (DEPTH, 2, SSM_HEADS)),
        'ssm_D': 1.0 + nrm(ks[15], (DEPTH, SSM_HEADS), 0.02),
        'ssm_norm_w': 1.0 + nrm(ks[16], (DEPTH, SSM_INNER), 0.02),
        'w_proj_gdn': nrm(ks[17], (DEPTH, GDN_V, D_MODEL), GDN_V ** -0.5),
        'w_proj_ssm': nrm(ks[18], (DEPTH, SSM_INNER, D_MODEL), SSM_INNER ** -0.5),
        'w_out': nrm(ks[19], (DEPTH, D_MODEL, D_MODEL), D_MODEL ** -0.5 * DEEPNORM_BETA),
        'ln1_g': 1.0 + nrm(ks[20], (DEPTH, D_MODEL), 0.02),
        'ln1_b': nrm(ks[21], (DEPTH, D_MODEL), 0.01),
        'w_ff1': nrm(ks[22], (DEPTH, D_MODEL, D_FF), D_MODEL ** -0.5),
        'b_ff1': nrm(ks[23], (DEPTH, D_FF), 0.01),
        'w_ff2': nrm(ks[24], (DEPTH, D_FF, D_MODEL), D_FF ** -0.5 * DEEPNORM_BETA),
        'b_ff2': nrm(ks[25], (DEPTH, D_MODEL), 0.01),
        'ln2_g': 1.0 + nrm(ks[26], (DEPTH, D_MODEL), 0.02),
        'ln2_b': nrm(ks[27], (DEPTH, D_MODEL), 0.01),
    }


def reference(x, c, ctx, c_ctx, w_mod, b_mod, w_in, gdn_conv_w, gdn_A_log, gdn_dt_bias, gdn_norm_w,
              ssm_conv_w, ssm_conv_b, ssm_A_log, ssm_dt_bias, ssm_D, ssm_norm_w,
              w_proj_gdn, w_proj_ssm, w_out, ln1_g, ln1_b, w_ff1, b_ff1, w_ff2, b_ff2, ln2_g, ln2_b):
    rows = x.shape[1] // GRID_W
    for l in range(DEPTH):
        last = l == DEPTH - 1
        mod = jax.nn.silu(c) @ w_mod[l] + b_mod[l]
        mod_c = jax.nn.silu(c_ctx) @ w_mod[l] + b_mod[l]
        sh1, sc1, g1, sh2, sc2, g2 = jnp.split(mod[:, None, :], 6, axis=-1)
        sh1c, sc1c, g1c, sh2c, sc2c, g2c = jnp.split(mod_c, 6)
        mix, mix_c = mixer(x * (1.0 + sc1) + sh1, ctx * (1.0 + sc1c) + sh1c, rows, w_in[l],
                           gdn_conv_w[l], gdn_A_log[l], gdn_dt_bias[l], gdn_norm_w[l],
                           ssm_conv_w[l], ssm_conv_b[l], ssm_A_log[l], ssm_dt_bias[l], ssm_D[l], ssm_norm_w[l],
                           w_proj_gdn[l], w_proj_ssm[l], w_out[l], not last)
        x = layer_norm(DEEPNORM_ALPHA * x + g1 * mix, ln1_g[l], ln1_b[l])
        x = layer_norm(DEEPNORM_ALPHA * x + g2 * mlp(x * (1.0 + sc2) + sh2, w_ff1[l], b_ff1[l], w_ff2[l], b_ff2[l]), ln2_g[l], ln2_b[l])
        if not last:
            ctx = layer_norm(DEEPNORM_ALPHA * ctx + g1c * mix_c, ln1_g[l], ln1_b[l])
            ctx = layer_norm(DEEPNORM_ALPHA * ctx + g2c * mlp(ctx * (1.0 + sc2c) + sh2c, w_ff1[l], b_ff1[l], w_ff2[l], b_ff2[l]), ln2_g[l], ln2_b[l])
    return x
```

```python
import numpy as np
import concourse.bass as bass
import concourse.mybir as mybir
from concourse.bass_utils import run_bass_kernel_spmd
from contextlib import ExitStack

F32 = mybir.dt.float32
BF16 = mybir.dt.bfloat16
AF = mybir.ActivationFunctionType
ALU = mybir.AluOpType
AX = mybir.AxisListType

ENGS = ("sync", "scalar", "vector", "gpsimd", "tensor")
DMA_POOL = 12

D = 1024
LC = 256
IN_DIM = 12384
C_QKV, C_GOUT, C_AB, C_Z, C_XBC, C_DT, C_GA, C_GB = 0, 3072, 4096, 4128, 6176, 10272, 10336, 11360
ALPHA = 4 ** 0.25
NEG = -30000.0


class Ins:
    __slots__ = ("eng", "fn", "dma", "deps", "flag", "sem", "val")

    def __init__(self, eng, fn, dma):
        self.eng = eng
        self.fn = fn
        self.dma = dma
        self.deps = []
        self.flag = False
        self.sem = None
        self.val = 0


class Prog:
    def __init__(self, nc):
        self.nc = nc
        self.lists = {e: [] for e in ENGS}
        self.last_w = {}
        self.readers = {}
        self.pending_dma = []
        self.n = 0

    def _need(self, ins, d):
        if d is None or d is ins:
            return
        if (not d.dma) and d.eng == ins.eng and d.eng == "tensor":
            return
        d.flag = True
        ins.deps.append(d)

    def _track(self, ins, r, w):
        for k in r:
            self._need(ins, self.last_w.get(k))
        for k in w:
            self._need(ins, self.last_w.get(k))
            rd = self.readers.get(k)
            if rd:
                for d in rd.values():
                    self._need(ins, d)
        for k in w:
            self.last_w[k] = ins
            self.readers[k] = {}
        for k in r:
            rd = self.readers.setdefault(k, {})
            if ins.dma:
                rd[("dma", id(ins))] = ins
            else:
                rd[ins.eng] = ins

    def op(self, eng, fn, r=(), w=()):
        ins = Ins(eng, fn, False)
        self._track(ins, r, w)
        self.lists[eng].append(ins)
        self.n += 1
        return ins

    def dma(self, eng, out, in_, r=(), w=(), **kw):
        ins = Ins(eng, lambda e: e.dma_start(out=out, in_=in_, **kw), True)
        self._track(ins, r, w)
        self.lists[eng].append(ins)
        self.pending_dma.append(ins)
        self.n += 1
        return ins

    def barrier(self):
        lasts = []
        for e in ENGS:
            lst = self.lists[e]
            for ins in reversed(lst):
                if not ins.dma:
                    lasts.append(ins)
                    break
        pend = self.pending_dma
        self.pending_dma = []
        for e in ENGS:
            ins = Ins(e, lambda eng: eng.nop(), False)
            for d in lasts:
                if d.eng != e:
                    d.flag = True
                    ins.deps.append(d)
            ins.deps.extend(pend)
            self.lists[e].append(ins)
        self.last_w = {}
        self.readers = {}

    def finish(self):
        nc = self.nc
        with ExitStack() as st:
            esem = {e: st.enter_context(nc.semaphore("s_" + e)) for e in ENGS}
            dsem = {e: [st.enter_context(nc.semaphore("d_%s%d" % (e, i))) for i in range(DMA_POOL)]
                    for e in ENGS if any(i.dma for i in self.lists[e])}
            for e in ENGS:
                cnt = 0
                nd = 0
                uses = [0] * DMA_POOL
                prev = [None] * DMA_POOL
                for ins in self.lists[e]:
                    if ins.dma:
                        k = nd % DMA_POOL
                        nd += 1
                        uses[k] += 1
                        ins.sem = dsem[e][k]
                        ins.val = 16 * uses[k]
                        if prev[k] is not None:
                            ins.deps.append(prev[k])
                        prev[k] = ins
                    elif ins.flag:
                        cnt += 1
                        ins.sem = esem[e]
                        ins.val = cnt
            with nc.Block() as block:
                for e in ENGS:
                    lst = self.lists[e]
                    if not lst:
                        continue

                    def body(eng, lst=lst):
                        seen = {}
                        for ins in lst:
                            need = {}
                            for d in ins.deps:
                                sid = id(d.sem)
                                if seen.get(sid, 0) >= d.val:
                                    continue
                                if sid not in need or need[sid][1] < d.val:
                                    need[sid] = (d.sem, d.val)
                            for sid, (sem, val) in need.items():
                                eng.wait_ge(sem, val)
                                seen[sid] = val
                            bi = ins.fn(eng)
                            if ins.dma:
                                bi.then_inc(ins.sem, 16)
                            elif ins.flag:
                                bi.then_inc(ins.sem, 1)

                    getattr(block, e)(body)


class Arena:
    def __init__(self, t32, ncols):
        self.t32 = t32
        self.t16 = t32.bitcast(BF16)
        self.ncols = ncols
        self.off = 0
        self.base = 0

    def f32(self, n):
        a = self.t32[:, self.off:self.off + n]
        self.off += n
        assert self.off <= self.ncols, ("SBUF arena overflow", self.off)
        return a

    def bf16(self, n):
        m = (n + 1) // 2
        a = self.t16[:, 2 * self.off:2 * self.off + n]
        self.off += m
        assert self.off <= self.ncols, ("SBUF arena overflow", self.off)
        return a

    def mark(self):
        self.base = self.off

    def reset(self):
        self.off = self.base


def v3(ap, a):
    return ap.rearrange("p (a b) -> p a b", a=a)


def bc_last(ap2, n):
    (ps, pc), (s, c) = ap2.ap
    return bass.AP(ap2.tensor, ap2.offset, [[ps, pc], [s, c], [0, n]])


def bc_mid(ap2, n):
    (ps, pc), (s, c) = ap2.ap
    return bass.AP(ap2.tensor, ap2.offset, [[ps, pc], [0, n], [s, c]])


K_ID, K_ONES, K_UF, K_UB, K_LF, K_LB, K_NSF, K_NSB, K_NIF, K_NIB = range(10)
NCONST = 10


def make_consts():
    p = np.arange(128)[:, None]
    i = np.arange(128)[None, :]
    c = np.zeros((128, NCONST, 128), np.float32)
    c[:, K_ID] = (p == i)
    c[:, K_ONES] = 1.0
    c[:, K_UF] = (p <= i)
    c[:, K_UB] = (p >= i)
    c[:, K_LF] = (p > i)
    c[:, K_LB] = (p < i)
    c[:, K_NSF] = np.where(p < i, 0.0, NEG)
    c[:, K_NSB] = np.where(p > i, 0.0, NEG)
    c[:, K_NIF] = np.where(p <= i, 0.0, NEG)
    c[:, K_NIB] = np.where(p >= i, 0.0, NEG)
    return c


def build_nc(GW=64, DEPTH=2, dbg=(), stop_after=None, force_ctx_out=False):
    L = 128 * GW
    T = L + LC
    NT = L // 512
    nc = bass.Bass("TRN2", target_bir_lowering=False)
    dbg = set(dbg)

    def IN(name, shape):
        return nc.dram_tensor(name, list(shape), F32, kind="ExternalInput").ap()

    def SCR(name, shape, dt=F32):
        kind = "ExternalOutput" if name in dbg else "Internal"
        return nc.dram_tensor(name, list(shape), dt, kind=kind).ap()

    x_in = IN("x", [L, D])
    ctx_in = IN("ctx", [LC, D])
    c_col = IN("c_col", [128, 8, 2])
    consts_in = IN("consts", [128, NCONST, 128])
    w_mod = IN("w_mod", [DEPTH, D, 6 * D])
    b_mod = IN("b_mod", [DEPTH, 6 * D])
    w_in = IN("w_in", [DEPTH, D, IN_DIM])
    gcw_col = IN("gcw_col", [128, DEPTH, 24, 5])
    scw_col = IN("scw_col", [128, DEPTH, 32, 5])
    scb_col = IN("scb_col", [128, DEPTH, 32])
    gdn_A_log = IN("gdn_A_log", [DEPTH, 16])
    gdn_dt_bias = IN("gdn_dt_bias", [DEPTH, 16])
    gdn_norm_w = IN("gdn_norm_w", [DEPTH, 128])
    ssm_A_log = IN("ssm_A_log", [DEPTH, 64])
    ssm_dt_bias = IN("ssm_dt_bias", [DEPTH, 64])
    ssm_D = IN("ssm_D", [DEPTH, 32])
    ssm_norm_w = IN("ssm_norm_w", [DEPTH, 2048])
    w_proj_gdn = IN("w_proj_gdn", [DEPTH, 1024, D])
    w_proj_ssm = IN("w_proj_ssm", [DEPTH, 2048, D])
    w_out = IN("w_out", [DEPTH, D, D])
    ln1_g = IN("ln1_g", [DEPTH, D])
    ln1_b = IN("ln1_b", [DEPTH, D])
    w_ff1 = IN("w_ff1", [DEPTH, D, 4 * D])
    bff1_col = IN("bff1_col", [128, DEPTH, 32])
    w_ff2 = IN("w_ff2", [DEPTH, 4 * D, D])
    b_ff2 = IN("b_ff2", [DEPTH, D])
    ln2_g = IN("ln2_g", [DEPTH, D])
    ln2_b = IN("ln2_b", [DEPTH, D])
    y_out = nc.dram_tensor("y", [L, D], F32, kind="ExternalOutput").ap()

    s_qT = SCR("s_qT", [8, 128, T])
    s_kT = SCR("s_kT", [8, 128, T])
    s_k = SCR("s_k", [8, T, 128])
    s_v = SCR("s_v", [8, T, 128])
    s_go = SCR("s_go", [T, 1024])
    s_gb = SCR("s_gb", [T, 32])
    s_ga = SCR("s_ga", [T, 1024])
    s_xs = SCR("s_xs", [T, 2048])
    s_B = SCR("s_B", [8, T, 128])
    s_BT = SCR("s_BT", [8, 128, T])
    s_CT = SCR("s_CT", [8, 128, T])
    s_dt = SCR("s_dt", [T, 128])
    s_z = SCR("s_z", [T, 2048])
    s_gbt = SCR("s_gbt", [T, 1024])
    s_of = SCR("s_of", [T, 1024])
    s_ob = SCR("s_ob", [T, 1024])
    s_yf = SCR("s_yf", [T, 2048])
    s_yb = SCR("s_yb", [T, 2048])
    s_pb = SCR("s_pb", [T, 1024])
    s_x1 = SCR("s_x1", [T, D])
    s_x2 = SCR("s_x2", [T, D])
    s_grow = SCR("s_grow", [128, 4096])

    with ExitStack() as st:
        ARENA_COLS = 47 * 1024
        arena_t = st.enter_context(nc.sbuf_tensor("arena", [128, ARENA_COLS], F32))
        A = Arena(arena_t, ARENA_COLS)
        PSB = [st.enter_context(nc.psum_tensor("psb%d" % i, [128, 512], F32)) for i in range(8)]
        p = Prog(nc)

        def MM(out, lhsT, rhs, s, e, r, w):
            p.op("tensor", lambda en: en.matmul(out, lhsT, rhs, start=s, stop=e), r=r, w=w)

        def TRP(out, in_, r, w):
            p.op("tensor", lambda en: en.transpose(out, in_, ident), r=list(r) + ["consts"], w=w)

        def ACT(out, in_, func, r, w, bias=None, scale=None, accum=None):
            kw = {}
            if bias is not None:
                kw["bias"] = bias
            if scale is not None:
                kw["scale"] = scale
            if accum is not None:
                kw["accum_out"] = accum
            p.op("scalar", lambda en: en.activation(out=out, in_=in_, func=func, **kw), r=r, w=w)

        def TS(eng, out, in0, s1, s2, op0, op1, r, w):
            if s2 is None:
                p.op(eng, lambda en: en.tensor_scalar(out, in0, s1, None, op0), r=r, w=w)
            else:
                p.op(eng, lambda en: en.tensor_scalar(out, in0, s1, s2, op0, op1), r=r, w=w)

        def TT(eng, out, in0, in1, op, r, w):
            p.op(eng, lambda en: en.tensor_tensor(out, in0, in1, op), r=r, w=w)

        def STT(out, in0, sc, in1, op0, op1, r, w):
            p.op("vector", lambda en: en.scalar_tensor_tensor(out, in0, sc, in1, op0, op1), r=r, w=w)

        def CP(eng, out, in_, r, w):
            if eng == "scalar":
                p.op(eng, lambda en: en.copy(out, in_), r=r, w=w)
            else:
                p.op(eng, lambda en: en.tensor_copy(out, in_), r=r, w=w)

        def RECIP(out, in_, r, w):
            p.op("vector", lambda en: en.reciprocal(out, in_), r=r, w=w)

        def DMA(eng, out, in_, r=(), w=(), **kw):
            p.dma(eng, out, in_, r=r, w=w, **kw)

        def MEMSET(eng, ap, val, w):
            p.op(eng, lambda en: en.memset(ap, val), w=w)

        consts = A.f32(NCONST * 128)
        cv = v3(consts, NCONST)
        ident = cv[:, K_ID, :]
        ones = cv[:, K_ONES, :]
        ccol = A.f32(16)
        epsc = A.f32(4)
        modcol = A.f32(48 * 2)
        modcv = v3(modcol, 48)
        cwg = A.f32(DEPTH * 24 * 5)
        cws = A.f32(DEPTH * 32 * 5)
        cbs = A.f32(DEPTH * 32)
        bf1 = A.f32(DEPTH * 32)
        A.mark()

        DMA("sync", consts, consts_in.rearrange("p a b -> p (a b)"), w=["consts"])
        DMA("sync", ccol, c_col.rearrange("p a b -> p (a b)"), w=["ccol"])
        DMA("sync", cwg, gcw_col.rearrange("p a b c -> p (a b c)"), w=["cwg"])
        DMA("sync", cws, scw_col.rearrange("p a b c -> p (a b c)"), w=["cws"])
        DMA("sync", cbs, scb_col.rearrange("p a b -> p (a b)"), w=["cbs"])
        DMA("sync", bf1, bff1_col.rearrange("p a b -> p (a b)"), w=["bf1"])
        MEMSET("vector", epsc[:, 0:1], 1e-6, w=["epsc"])
        MEMSET("vector", epsc[:, 1:2], 1e-5, w=["epsc"])
        ACT(ccol, ccol, AF.Silu, r=["ccol"], w=["ccol"])
        cwgv = cwg.rearrange("p (a b c) -> p a b c", a=DEPTH, b=24)
        cwsv = cws.rearrange("p (a b c) -> p a b c", a=DEPTH, b=32)
        cbsv = v3(cbs, DEPTH)
        bf1v = v3(bf1, DEPTH)
        ccv = v3(ccol, 8)

        rr = [0]

        def rot(engs):
            rr[0] += 1
            return engs[rr[0] % len(engs)]

        def stage0(l):
            A.reset()
            wst = [A.f32(8 * 512) for _ in range(2)]
            brow = A.f32(6 * D)
            grow = A.f32(4 * 1024)
            growv = v3(grow, 4)
            DMA("sync", brow[0:1, :], b_mod[l:l + 1, :], w=["brow"])
            for blk in range(12):
                wt = wst[blk % 2]
                wk = "wst%d" % (blk % 2)
                wv = v3(wt, 8)
                DMA(rot(["sync", "gpsimd"]), wv, w_mod[l, :, blk * 512:(blk + 1) * 512].rearrange("(kt p) c -> p kt c", p=128), w=[wk])
                for sub in range(4):
                    j = blk * 4 + sub
                    pst = PSB[sub % 2]
                    pk = "ps%d" % (sub % 2)
                    for kt in range(8):
                        MM(pst[:, 0:2], wv[:, kt, sub * 128:(sub + 1) * 128], ccv[:, kt, :], kt == 0, False, r=[wk, "ccol"], w=[pk])
                    MM(pst[:, 0:2], brow[0:1, j * 128:(j + 1) * 128], ones[0:1, 0:2], False, True, r=["brow", "consts"], w=[pk])
                    addv = 1.0 if (8 <= j < 16 or 32 <= j < 40) else 0.0
                    TS("vector", modcv[:, j, :], pst[:, 0:2], addv, None, ALU.add, None, r=[pk], w=["modcol"])
                if blk in (4, 5, 10, 11):
                    gi = 0 if blk in (4, 5) else 2
                    half = blk % 2
                    for which in range(2):
                        pst = PSB[2 + which]
                        pk = "ps%d" % (2 + which)
                        for kt in range(8):
                            lhs = bass.AP(ccol.tensor, ccv[:, kt, which:which + 1].offset, [list(ccol.ap[0]), [0, 128]])
                            MM(pst[:, :], lhs, wv[:, kt, :], kt == 0, False, r=[wk, "ccol"], w=[pk])
                        MM(pst[:, :], ones[0:1, :], brow[0:1, blk * 512:(blk + 1) * 512], False, True, r=["brow", "consts"], w=[pk])
                        CP("scalar", growv[:, gi + which, half * 512:(half + 1) * 512], pst[:, :], r=[pk], w=["grow"])
            DMA("sync", s_grow, grow, r=["grow"])

        def load_wblock(l, wsrc, ranges, wbf, key):
            stg = [A.f32(8 * 512) for _ in range(2)]
            o = 0
            i = 0
            for (c0, n) in ranges:
                for cc in range(0, n, 512):
                    m = min(512, n - cc)
                    sk = "wstg%d" % (i % 2)
                    sv = v3(stg[i % 2], 8)
                    DMA(rot(["sync", "gpsimd"]), sv[:, :, 0:m], wsrc[:, c0 + cc:c0 + cc + m].rearrange("(kt p) c -> p kt c", p=128), w=[sk])
                    CP(rot(["vector", "gpsimd", "scalar"]), wbf[:, :, o:o + m], sv[:, :, 0:m], r=[sk], w=[key])
                    o += m
                    i += 1

        def xrows(l, order, tile):
            if tile == NT:
                src = ctx_in if l == 0 else xprev[L:T, :]
                return [src[s * 128:(s + 1) * 128, :] for s in range(2)]
            src = x_in if l == 0 else xprev[0:L, :]
            if order == "r":
                return [src[tile * 512 + s * 128: tile * 512 + (s + 1) * 128, :] for s in range(4)]
            colv = src.rearrange("(r c) d -> c r d", c=GW)
            return [colv[tile * 4 + s] for s in range(4)]

        def proj_pass(l, order, specs, wranges):
            A.reset()
            ncols = sum(n for _, n in wranges)
            wbf_flat = A.bf16(8 * ncols)
            wbf = v3(wbf_flat, 8)
            mk = A.off
            load_wblock(l, w_in[l], wranges, wbf, "wbf")
            A.off = mk
            p.barrier()
            xts = [A.f32(4 * 1024) for _ in range(2)]
            hts = [A.bf16(8 * 512) for _ in range(2)]
            ctxt = {"wbf": wbf, "l": l, "order": order}
            for sp in specs:
                sp["init"](ctxt)
            for tile in range(NT + 1):
                isctx = tile == NT
                nsub = 2 if isctx else 4
                ntok = nsub * 128
                t0 = L if isctx else tile * 512
                xt = xts[tile % 2]
                xk = "xt%d" % (tile % 2)
                xv = v3(xt, 4)
                for s, src in enumerate(xrows(l, order, tile)):
                    DMA(rot(["sync", "gpsimd"]), xv[:, s, :], src, w=[xk])
                ht = hts[tile % 2]
                hk = "ht%d" % (tile % 2)
                hv = v3(ht, 8)
                mi = 1 if isctx else 0
                for kt in range(8):
                    pst = PSB[kt % 2]
                    pk = "ps%d" % (kt % 2)
                    for s in range(nsub):
                        TRP(pst[:, s * 128:(s + 1) * 128], xv[:, s, kt * 128:(kt + 1) * 128], r=[xk], w=[pk])
                    if kt % 2 == 0:
                        ACT(hv[:, kt, 0:ntok], pst[:, 0:ntok], AF.Identity, r=[pk, "modcol"], w=[hk],
                            bias=modcv[:, 0 + kt, mi:mi + 1], scale=modcv[:, 8 + kt, mi:mi + 1])
                    else:
                        TS("vector", hv[:, kt, 0:ntok], pst[:, 0:ntok], modcv[:, 8 + kt, mi:mi + 1], modcv[:, 0 + kt, mi:mi + 1],
                           ALU.mult, ALU.add, r=[pk, "modcol"], w=[hk])
                for sp in specs:
                    sp["run"](ctxt, hv, hk, t0, nsub, isctx)

        def spec_tm(woff, ncols, func, dst, dcol=0):
            stg = {}

            def init(cx):
                stg["t"] = A.f32(4 * ncols)

            def run(cx, hv, hk, t0, nsub, isctx):
                wbf = cx["wbf"]
                tv = v3(stg["t"], 4)
                for s in range(nsub):
                    tk = "tm%d_%d_%d" % (woff, dcol, s)
                    for cb in range(0, ncols, 512):
                        pst = PSB[2 + (s + cb // 512) % 2]
                        pk = "ps%d" % (2 + (s + cb // 512) % 2)
                        for kt in range(8):
                            MM(pst[:, :], hv[:, kt, s * 128:(s + 1) * 128], wbf[:, kt, woff + cb:woff + cb + 512], kt == 0, kt == 7,
                               r=[hk, "wbf"], w=[pk])
                        ACT(tv[:, s, cb:cb + 512], pst[:, :], func, r=[pk], w=[tk])
                    DMA(rot(["sync", "gpsimd"]), dst[t0 + s * 128:t0 + (s + 1) * 128, dcol:dcol + ncols], tv[:, s, :], r=[tk])

            return {"init": init, "run": run}

        def spec_small(woff, nh, a_log_in, dt_bias_in, dst, is_gdn):
            stg = {}
            n2 = 2 * nh
            nw = n2 if is_gdn else nh

            def init(cx):
                l = cx["l"]
                stg["t"] = [A.f32(4 * n2) for _ in range(2)]
                stg["tmp"] = [A.f32(4 * nh) for _ in range(2)]
                stg["i"] = 0
                stg["nega"] = A.f32(nh)
                stg["dtb"] = A.f32(nh)
                DMA("sync", stg["nega"], a_log_in[l:l + 1, :].partition_broadcast(128), w=["nega"])
                DMA("sync", stg["dtb"], dt_bias_in[l:l + 1, :].partition_broadcast(128), w=["dtb"])
                ACT(stg["nega"], stg["nega"], AF.Exp, r=["nega"], w=["nega"])
                TS("vector", stg["nega"], stg["nega"], -1.0, None, ALU.mult, None, r=["nega"], w=["nega"])

            def run(cx, hv, hk, t0, nsub, isctx):
                wbf = cx["wbf"]
                i = stg["i"]
                stg["i"] += 1
                tt = stg["t"][i % 2]
                tk = "sm%d_%d" % (woff, i % 2)
                tv = v3(tt, 4)
                tmp = stg["tmp"][i % 2]
                mk = "smt%d_%d" % (woff, i % 2)
                mv = v3(tmp, 4)
                pst = PSB[4]
                pk = "ps4"
                pv = v3(pst[:, 0:4 * n2], 4)
                for s in range(nsub):
                    for kt in range(8):
                        MM(pv[:, s, 0:nw], hv[:, kt, s * 128:(s + 1) * 128], wbf[:, kt, woff:woff + nw], kt == 0, kt == 7, r=[hk, "wbf"], w=[pk])
                ns = nsub
                if is_gdn:
                    TT("vector", mv[:, 0:ns, :], pv[:, 0:ns, 0:nh], bc_mid(stg["dtb"], ns), ALU.add, r=[pk, "dtb"], w=[mk])
                    ACT(mv[:, 0:ns, :], mv[:, 0:ns, :], AF.Exp, r=[mk], w=[mk])
                    ACT(mv[:, 0:ns, :], mv[:, 0:ns, :], AF.Ln, r=[mk], w=[mk], bias=1.0)
                    TT("vector", tv[:, 0:ns, 0:nh], mv[:, 0:ns, :], bc_mid(stg["nega"], ns), ALU.mult, r=[mk, "nega"], w=[tk])
                    ACT(tv[:, 0:ns, nh:n2], pv[:, 0:ns, nh:n2], AF.Sigmoid, r=[pk], w=[tk])
                else:
                    TT("vector", mv[:, 0:ns, :], pv[:, 0:ns, 0:nh], bc_mid(stg["dtb"], ns), ALU.add, r=[pk, "dtb"], w=[mk])
                    ACT(mv[:, 0:ns, :], mv[:, 0:ns, :], AF.Exp, r=[mk], w=[mk])
                    ACT(tv[:, 0:ns, 0:nh], mv[:, 0:ns, :], AF.Ln, r=[mk], w=[tk], bias=1.0)
                    TT("vector", tv[:, 0:ns, nh:n2], tv[:, 0:ns, 0:nh], bc_mid(stg["nega"], ns), ALU.mult, r=[tk, "nega"], w=[tk])
                DMA(rot(["sync", "gpsimd"]), dst[t0:t0 + ns * 128, :].rearrange("(s p) c -> p s c", p=128), tv[:, 0:ns, :], r=[tk])

            return {"init": init, "run": run}

        def spec_fm(woff, nchunks, convw, convb, post):
            stg = {}

            def init(cx):
                stg["acc"] = [A.f32(512) for _ in range(2)]
                stg["o"] = [A.f32(512) for _ in range(3)]
                stg["sq"] = [A.f32(512) for _ in range(2)]
                stg["tm"] = [A.f32(512) for _ in range(3)]
                stg["i"] = 0

            def run(cx, hv, hk, t0, nsub, isctx):
                wbf = cx["wbf"]
                l = cx["l"]
                ntok = nsub * 128
                if isctx:
                    G, W = 1, 256
                elif cx["order"] == "r":
                    G, W = 512 // GW, GW
                else:
                    G, W = 4, 128
                for j in range(nchunks):
                    i = stg["i"]
                    stg["i"] += 1
                    pst = PSB[2 + i % 2]
                    pk = "ps%d" % (2 + i % 2)
                    for kt in range(8):
                        MM(pst[:, 0:ntok], wbf[:, kt, woff + j * 128:woff + (j + 1) * 128], hv[:, kt, 0:ntok], kt == 0, kt == 7, r=[hk, "wbf"], w=[pk])
                    acc = stg["acc"][i % 2]
                    ak = "acc%d_%d" % (woff, i % 2)
                    pv = v3(pst[:, 0:ntok], G)
                    av = v3(acc[:, 0:ntok], G)
                    cwv = convw[:, l, j, :]
                    ACT(acc[:, 0:ntok], pst[:, 0:ntok], AF.Identity, r=[pk, "cw"], w=[ak], scale=cwv[:, 2:3])
                    for k in (0, 1, 3, 4):
                        d = k - 2
                        lo_o, hi_o = max(0, -d), W - max(0, d)
                        lo_i, hi_i = max(0, d), W - max(0, -d)
                        STT(av[:, :, lo_o:hi_o], pv[:, :, lo_i:hi_i], cwv[:, k:k + 1], av[:, :, lo_o:hi_o], ALU.mult, ALU.add, r=[pk, ak, "cw"], w=[ak])
                    o = stg["o"][i % 3]
                    ok = "fo%d_%d" % (woff, i % 3)
                    if convb is not None:
                        ACT(o[:, 0:ntok], acc[:, 0:ntok], AF.Silu, r=[ak, "cw"], w=[ok], bias=convb[:, l, j:j + 1])
                    else:
                        ACT(o[:, 0:ntok], acc[:, 0:ntok], AF.Silu, r=[ak], w=[ok])
                    post(cx, stg, j, i, o, ok, t0, nsub)

            return {"init": init, "run": run}

        def store_fm(dst3, idx, o, ok, t0, ntok):
            DMA(rot(["sync", "gpsimd"]), dst3[idx][:, t0:t0 + ntok], o[:, 0:ntok], r=[ok])

        def store_tm(stg, i, dst3, idx, o, ok, t0, nsub, col0=None):
            pst = PSB[5 + i % 2]
            pk = "ps%d" % (5 + i % 2)
            for s in range(nsub):
                TRP(pst[:, s * 128:(s + 1) * 128], o[:, s * 128:(s + 1) * 128], r=[ok], w=[pk])
            tm = stg["tm"][i % 3]
            tk = "tmo%d" % (i % 3)
            CP(rot(["vector", "scalar"]), tm[:, 0:nsub * 128], pst[:, 0:nsub * 128], r=[pk], w=[tk])
            if col0 is None:
                d = dst3[idx][t0:t0 + nsub * 128, :]
            else:
                d = dst3[t0:t0 + nsub * 128, col0:col0 + 128]
            DMA(rot(["sync", "gpsimd"]), d.rearrange("(s p) c -> p s c", p=128), v3(tm[:, 0:nsub * 128], nsub), r=[tk])

        def l2norm(stg, i, o, ok, ntok, scale):
            sq = stg["sq"][i % 2]
            sk = "sq%d" % (i % 2)
            TT("gpsimd", sq[:, 0:ntok], o[:, 0:ntok], o[:, 0:ntok], ALU.mult, r=[ok], w=[sk])
            pst = PSB[7]
            MM(pst[:, 0:ntok], ones, sq[:, 0:ntok], True, True, r=[sk, "consts"], w=["ps7"])
            ACT(sq[:, 0:ntok], pst[:, 0:ntok], AF.Sqrt, r=["ps7", "epsc"], w=[sk], bias=epsc[:, 0:1], scale=1.0)
            RECIP(sq[:, 0:ntok], sq[:, 0:ntok], r=[sk], w=[sk])
            if scale == 1.0:
                TT("vector", o[:, 0:ntok], o[:, 0:ntok], sq[:, 0:ntok], ALU.mult, r=[ok, sk], w=[ok])
            else:
                STT(o[:, 0:ntok], o[:, 0:ntok], scale, sq[:, 0:ntok], ALU.mult, ALU.mult, r=[ok, sk], w=[ok])

        def post_qkv(cx, stg, j, i, o, ok, t0, nsub):
            ntok = nsub * 128
            if j < 8:
                l2norm(stg, i, o, ok, ntok, 128 ** -0.5)
                store_fm(s_qT, j, o, ok, t0, ntok)
            elif j < 16:
                l2norm(stg, i, o, ok, ntok, 1.0)
                store_fm(s_kT, j - 8, o, ok, t0, ntok)
                store_tm(stg, i, s_k, j - 8, o, ok, t0, nsub)
            else:
                store_tm(stg, i, s_v, j - 16, o, ok, t0, nsub)

        def post_xbc(cx, stg, j, i, o, ok, t0, nsub):
            ntok = nsub * 128
            if j < 16:
                store_tm(stg, i, s_xs, None, o, ok, t0, nsub, col0=j * 128)
            elif j < 24:
                store_fm(s_BT, j - 16, o, ok, t0, ntok)
                store_tm(stg, i, s_B, j - 16, o, ok, t0, nsub)
            else:
                store_fm(s_CT, j - 24, o, ok, t0, ntok)

        def stage1(l):
            proj_pass(l, "r",
                      [spec_fm(0, 24, cwgv, None, post_qkv),
                       spec_small(3072, 16, gdn_A_log, gdn_dt_bias, s_gb, True)],
                      [(C_QKV, 3072), (C_AB, 32)])
            p.barrier()
            proj_pass(l, "r",
                      [spec_tm(0, 1024, AF.Silu, s_go),
                       spec_tm(1024, 1024, AF.Sigmoid, s_ga)],
                      [(C_GOUT, 1024), (C_GA, 1024)])
            p.barrier()
            proj_pass(l, "c",
                      [spec_fm(0, 32, cwsv, cbsv, post_xbc),
                       spec_small(4096, 64, ssm_A_log, ssm_dt_bias, s_dt, False)],
                      [(C_XBC, 4096), (C_DT, 64)])
            p.barrier()
            proj_pass(l, "c",
                      [spec_tm(0, 1024, AF.Silu, s_z, 0),
                       spec_tm(1024, 1024, AF.Silu, s_z, 1024),
                       spec_tm(2048, 1024, AF.Sigmoid, s_gbt)],
                      [(C_Z, 2048), (C_GB, 1024)])
            p.barrier()


        def stage_gdn(l, with_ctx_out):
            A.reset()
            NCH = L // 128
            seq = {0: [L, L + 128] + [c * 128 for c in range(NCH)],
                   1: [L + 128, L] + [c * 128 for c in reversed(range(NCH))]}
            nsteps = NCH + 2
            ld = {}
            for d in (0, 1):
                for b in (0, 1):
                    ld[(d, b)] = dict(qT=A.f32(1024), kT=A.f32(1024), k=A.f32(1024), v=A.f32(1024), gb=A.f32(32))
            Sb = {(d, b): A.f32(1024) for d in (0, 1) for b in (0, 1)}
            Gb = {(d, hg): [A.f32(512) for _ in range(11)] for d in (0, 1) for hg in (0, 1)}
            sm = {(d, hg): A.f32(12) for d in (0, 1) for hg in (0, 1)}
            negs4 = {d: A.f32(512) for d in (0, 1)}
            for d in (0, 1):
                MEMSET("gpsimd", Sb[(d, 0)], 0.0, w=["S%d_0_0" % d, "S%d_0_1" % d])
                for a in range(4):
                    CP("vector", negs4[d][:, a * 128:(a + 1) * 128], cv[:, K_NSF + d, :], r=["consts"], w=["negs4_%d" % d])

            def loads(d, s):
                t0 = seq[d][s]
                b = s % 2
                bf = ld[(d, b)]
                kk = "ld%d_%d_" % (d, b)
                DMA("sync", v3(bf["qT"], 8), s_qT[:, :, t0:t0 + 128].rearrange("h p t -> p h t"), w=[kk + "qT"])
                DMA("gpsimd", v3(bf["kT"], 8), s_kT[:, :, t0:t0 + 128].rearrange("h p t -> p h t"), w=[kk + "kT"])
                DMA("sync", v3(bf["k"], 8), s_k[:, t0:t0 + 128, :].rearrange("h t c -> t h c"), w=[kk + "k"])
                DMA("gpsimd", v3(bf["v"], 8), s_v[:, t0:t0 + 128, :].rearrange("h t c -> t h c"), w=[kk + "v"])
                DMA("sync", bf["gb"], s_gb[t0:t0 + 128, :], w=[kk + "gb"])

            def unit(d, hg, s):
                t0 = seq[d][s]
                b = s % 2
                need_out = (t0 < L) or with_ctx_out
                bf = ld[(d, b)]
                kk = "ld%d_%d_" % (d, b)
                gi = d * 2 + hg
                banks = [(PSB[2 * gi], "ps%d" % (2 * gi)), (PSB[2 * gi + 1], "ps%d" % (2 * gi + 1))]
                (bkA, kA), (bkB, kB) = banks
                hs = slice(hg * 4, hg * 4 + 4)
                U = cv[:, K_UF + d, :]
                Lm = cv[:, K_LF + d, :]
                g4 = bf["gb"][:, d * 8 + hg * 4:d * 8 + hg * 4 + 4]
                beta4 = bf["gb"][:, 16 + d * 8 + hg * 4:16 + d * 8 + hg * 4 + 4]
                B = Gb[(d, hg)]
                bk = ["G%d_%d" % (gi, i) for i in range(11)]
                T_gU, T_Es, T_Ei, T_B0, T_BT0, T_X1, T_XT1, T_P, T_nWT, T_vn, T_o = range(11)
                smt = sm[(d, hg)]
                smk = "sm%d" % gi
                qT3 = v3(bf["qT"], 8)
                kT3 = v3(bf["kT"], 8)
                k3 = v3(bf["k"], 8)
                vv3 = v3(bf["v"], 8)
                S_old = v3(Sb[(d, b)], 8)
                S_new = v3(Sb[(d, 1 - b)], 8)
                sk_old = "S%d_%d_%d" % (d, b, hg)
                sk_new = "S%d_%d_%d" % (d, 1 - b, hg)
                id4 = bc_mid(ident, 4)
                e4 = smt[:, 0:4]
                egl4 = smt[:, 4:8]
                dk4 = smt[:, 8:12]
                V = lambda i: v3(B[i], 4)
                TT("gpsimd", V(T_gU), bc_mid(U, 4), bc_last(g4, 128), ALU.mult, r=[kk + "gb", "consts"], w=[bk[T_gU]])
                yield
                MM(bkA[:, :], Lm, B[T_gU], True, False, r=[bk[T_gU], "consts"], w=[kA])
                MM(bkA[:, :], ident, negs4[d], False, True, r=["consts", "negs4_%d" % d], w=[kA])
                MM(bkB[:, 0:4], U, g4, True, True, r=[kk + "gb", "consts"], w=[kB])
                MM(bkB[:, 4:8], ones, g4, True, True, r=[kk + "gb", "consts"], w=[kB])
                MM(bkB[:, 8:12], Lm, g4, True, True, r=[kk + "gb", "consts"], w=[kB])
                yield
                ACT(B[T_Es], bkA[:, :], AF.Exp, r=[kA], w=[bk[T_Es]])
                ACT(smt[:, 0:12], bkB[:, 0:12], AF.Exp, r=[kB], w=[smk])
                yield
                for a in range(4):
                    h = hg * 4 + a
                    MM(bkA[:, a * 128:(a + 1) * 128], kT3[:, h, :], kT3[:, h, :], True, True, r=[kk + "kT"], w=[kA])
                if need_out:
                    for a in range(4):
                        h = hg * 4 + a
                        MM(bkB[:, a * 128:(a + 1) * 128], kT3[:, h, :], qT3[:, h, :], True, True, r=[kk + "kT", kk + "qT"], w=[kB])
                yield
                TT("vector", B[T_B0], bkA[:, :], B[T_Es], ALU.mult, r=[kA, bk[T_Es]], w=[bk[T_B0]])
                if need_out:
                    TT("gpsimd", V(T_Ei), V(T_Es), id4, ALU.add, r=[bk[T_Es], "consts"], w=[bk[T_Ei]])
                    TT("vector", B[T_Ei], bkB[:, :], B[T_Ei], ALU.mult, r=[kB, bk[T_Ei]], w=[bk[T_Ei]])
                TT("gpsimd", V(T_B0), V(T_B0), bc_last(beta4, 128), ALU.mult, r=[bk[T_B0], kk + "gb"], w=[bk[T_B0]])
                TT("gpsimd", V(T_P), id4, V(T_B0), ALU.subtract, r=[bk[T_B0], "consts"], w=[bk[T_P]])
                TT("gpsimd", V(T_gU), k3[:, hs, :], bc_last(e4, 128), ALU.mult, r=[kk + "k", smk], w=[bk[T_gU]])
                yield
                for a in range(4):
                    TRP(bkA[:, a * 128:(a + 1) * 128], B[T_B0][:, a * 128:(a + 1) * 128], r=[bk[T_B0]], w=[kA])
                yield
                CP("scalar", B[T_BT0], bkA[:, :], r=[kA], w=[bk[T_BT0]])
                TT("gpsimd", V(T_Es), k3[:, hs, :], bc_last(dk4, 128), ALU.mult, r=[kk + "k", smk, bk[T_Ei] if need_out else bk[T_B0]], w=[bk[T_Es]])
                yield
                X, XT = T_B0, T_BT0
                for m in range(1, 7):
                    nX = T_X1 if X == T_B0 else T_B0
                    nXT = T_XT1 if XT == T_BT0 else T_BT0
                    if m < 6:
                        for a in range(4):
                            sl = slice(a * 128, (a + 1) * 128)
                            MM(bkA[:, sl], B[XT][:, sl], B[X][:, sl], True, True, r=[bk[X], bk[XT]], w=[kA])
                    for a in range(4):
                        sl = slice(a * 128, (a + 1) * 128)
                        MM(bkB[:, sl], B[X][:, sl], B[XT][:, sl], True, True, r=[bk[X], bk[XT]], w=[kB])
                    yield
                    if m < 6:
                        CP("scalar", B[nX], bkA[:, :], r=[kA], w=[bk[nX]])
                    CP("vector", B[nXT], bkB[:, :], r=[kB], w=[bk[nXT]])
                    yield
                    for a in range(4):
                        sl = slice(a * 128, (a + 1) * 128)
                        MM(bkA[:, sl], B[nXT][:, sl], B[T_P][:, sl], True, True, r=[bk[nXT], bk[T_P]], w=[kA])
                    yield
                    TT("vector", B[T_P], bkA[:, :], B[T_P], ALU.add, r=[kA, bk[T_P]], w=[bk[T_P]])
                    yield
                    X, XT = nX, nXT
                for a in range(4):
                    sl = slice(a * 128, (a + 1) * 128)
                    MM(bkA[:, sl], B[T_gU][:, sl], B[T_P][:, sl], True, True, r=[bk[T_gU], bk[T_P]], w=[kA])
                yield
                ACT(B[T_nWT], bkA[:, :], AF.Identity, r=[kA], w=[bk[T_nWT]], scale=-1.0)
                yield
                for a in range(4):
                    h = hg * 4 + a
                    sl = slice(a * 128, (a + 1) * 128)
                    MM(bkB[:, sl], B[T_P][:, sl], vv3[:, h, :], True, False, r=[bk[T_P], kk + "v"], w=[kB])
                    MM(bkB[:, sl], B[T_nWT][:, sl], S_old[:, h, :], False, True, r=[bk[T_nWT], sk_old], w=[kB])
                yield
                TT("vector", V(T_vn), v3(bkB[:, :], 4), bc_last(beta4, 128), ALU.mult, r=[kB, kk + "gb"], w=[bk[T_vn]])
                yield
                for a in range(4):
                    sl = slice(a * 128, (a + 1) * 128)
                    MM(bkA[:, sl], B[T_Es][:, sl], B[T_vn][:, sl], True, True, r=[bk[T_Es], bk[T_vn]], w=[kA])
                if need_out:
                    for a in range(4):
                        h = hg * 4 + a
                        sl = slice(a * 128, (a + 1) * 128)
                        MM(bkB[:, sl], qT3[:, h, :], S_old[:, h, :], True, True, r=[kk + "qT", sk_old], w=[kB])
                yield
                TT("gpsimd", S_new[:, hs, :], S_old[:, hs, :], bc_last(egl4, 128), ALU.mult, r=[sk_old, smk], w=[sk_new])
                TT("vector", S_new[:, hs, :], v3(bkA[:, :], 4), S_new[:, hs, :], ALU.add, r=[kA, sk_new], w=[sk_new])
                if need_out:
                    TT("vector", V(T_o), v3(bkB[:, :], 4), bc_last(e4, 128), ALU.mult, r=[kB, smk], w=[bk[T_o]])
                    yield
                    for a in range(4):
                        sl = slice(a * 128, (a + 1) * 128)
                        MM(bkA[:, sl], B[T_Ei][:, sl], B[T_vn][:, sl], True, True, r=[bk[T_Ei], bk[T_vn]], w=[kA])
                    yield
                    TT("vector", B[T_o], bkA[:, :], B[T_o], ALU.add, r=[kA, bk[T_o]], w=[bk[T_o]])
                    dst = (s_of, s_ob)[d]
                    DMA(rot(["sync", "gpsimd"]), dst[t0:t0 + 128, hg * 512:(hg + 1) * 512], B[T_o], r=[bk[T_o]])
                yield

            for d in (0, 1):
                loads(d, 0)
            for s in range(nsteps):
                if s + 1 < nsteps:
                    for d in (0, 1):
                        loads(d, s + 1)
                gens = [unit(d, hg, s) for d in (0, 1) for hg in (0, 1)]
                alive = True
                while alive:
                    alive = False
                    for g in gens:
                        try:
                            next(g)
                            alive = True
                        except StopIteration:
                            pass


        def stage_ssd(l, with_ctx_out):
            A.reset()
            NCH = GW
            seq = {0: [L, L + 128] + [c * 128 for c in range(NCH)],
                   1: [L + 128, L] + [c * 128 for c in reversed(range(NCH))]}
            nsteps = NCH + 2
            ld = [dict(xs=A.f32(2048), B=A.f32(1024), BT=A.f32(1024), CT=A.f32(1024), dtla=A.f32(128)) for _ in range(2)]
            xdt = A.f32(2048)
            xdtd = A.f32(2048)
            ybuf = [A.f32(2048) for _ in range(2)]
            S = A.f32(2048)
            S3 = v3(S, 32)
            smt = A.f32(96)
            UB = [[A.f32(512) for _ in range(4)] for _ in range(4)]
            negi4 = A.f32(512)

            def loads(d, s):
                t0 = seq[d][s]
                bf = ld[s % 2]
                kk = "ld%d_" % (s % 2)
                DMA("sync", bf["xs"], s_xs[t0:t0 + 128, :], w=[kk + "xs"])
                DMA("gpsimd", v3(bf["B"], 8), s_B[:, t0:t0 + 128, :].rearrange("g t c -> t g c"), w=[kk + "B"])
                DMA("sync", v3(bf["BT"], 8), s_BT[:, :, t0:t0 + 128].rearrange("g p t -> p g t"), w=[kk + "BT"])
                DMA("gpsimd", v3(bf["CT"], 8), s_CT[:, :, t0:t0 + 128].rearrange("g p t -> p g t"), w=[kk + "CT"])
                DMA("sync", bf["dtla"], s_dt[t0:t0 + 128, :], w=[kk + "dtla"])

            def unit(d, u, s, yb, yk):
                t0 = seq[d][s]
                bf = ld[s % 2]
                kk = "ld%d_" % (s % 2)
                need_out = (t0 < L) or with_ctx_out
                (bkA, kA), (bkB, kB) = (PSB[2 * u], "ps%d" % (2 * u)), (PSB[2 * u + 1], "ps%d" % (2 * u + 1))
                U = cv[:, K_UF + d, :]
                Lm = cv[:, K_LF + d, :]
                la32 = bf["dtla"][:, 64 + d * 32:64 + d * 32 + 32]
                Bt3 = v3(bf["B"], 8)
                BT3 = v3(bf["BT"], 8)
                CT3 = v3(bf["CT"], 8)
                xdt3 = v3(xdt, 32)
                xdtd3 = v3(xdtd, 32)
                bufs = UB[u]
                bk = ["U%d_%d" % (u, i) for i in range(4)]
                hs8 = slice(u * 8, u * 8 + 8)
                e8 = smt[:, u * 8:u * 8 + 8]
                egl8 = smt[:, 32 + u * 8:32 + u * 8 + 8]
                if need_out:
                    for gg in range(2):
                        g = 2 * u + gg
                        TT("gpsimd", v3(bufs[gg], 4), bc_mid(U, 4), bc_last(la32[:, g * 4:g * 4 + 4], 128), ALU.mult, r=[kk + "dtla", "consts"], w=[bk[gg]])
                    yield
                    for gg, (bank, bkey) in enumerate(((bkA, kA), (bkB, kB))):
                        MM(bank[:, :], Lm, bufs[gg], True, False, r=[bk[gg], "consts"], w=[bkey])
                        MM(bank[:, :], ident, negi4, False, True, r=["consts", "negi4"], w=[bkey])
                    yield
                    ACT(bufs[2], bkA[:, :], AF.Exp, r=[kA], w=[bk[2]])
                    ACT(bufs[3], bkB[:, :], AF.Exp, r=[kB], w=[bk[3]])
                    yield
                    for gg in range(2):
                        g = 2 * u + gg
                        MM(bkA[:, gg * 128:(gg + 1) * 128], BT3[:, g, :], CT3[:, g, :], True, True, r=[kk + "BT", kk + "CT"], w=[kA])
                    yield
                    for gg in range(2):
                        TT("vector", v3(bufs[2 + gg], 4), v3(bufs[2 + gg], 4), bc_mid(bkA[:, gg * 128:(gg + 1) * 128], 4), ALU.mult, r=[kA, bk[2 + gg]], w=[bk[2 + gg]])
                    yield
                    for gg in range(2):
                        g = 2 * u + gg
                        for a in range(4):
                            h = g * 4 + a
                            MM(bkB[:, (gg * 4 + a) * 64:(gg * 4 + a + 1) * 64], bufs[2 + gg][:, a * 128:(a + 1) * 128], xdt3[:, h, :], True, True,
                               r=[bk[2 + gg], "xdt"], w=[kB])
                    for gg in range(2):
                        g = 2 * u + gg
                        MM(bkA[:, gg * 256:(gg + 1) * 256], CT3[:, g, :], S[:, g * 256:(g + 1) * 256], True, True, r=[kk + "CT", "S%d" % u], w=[kA])
                    yield
                    TT("vector", v3(yb[:, u * 512:(u + 1) * 512], 8), v3(bkA[:, :], 8), bc_last(e8, 64), ALU.mult, r=[kA, "smt"], w=[yk + "_%d" % u])
                    TT("vector", yb[:, u * 512:(u + 1) * 512], bkB[:, :], yb[:, u * 512:(u + 1) * 512], ALU.add, r=[kB, yk + "_%d" % u], w=[yk + "_%d" % u])
                    yield
                for gg in range(2):
                    g = 2 * u + gg
                    MM(bkA[:, gg * 256:(gg + 1) * 256], Bt3[:, g, :], xdtd[:, g * 256:(g + 1) * 256], True, True, r=[kk + "B", "xdtd"], w=[kA])
                yield
                TT("gpsimd", S3[:, hs8, :], S3[:, hs8, :], bc_last(egl8, 64), ALU.mult, r=["S%d" % u, "smt"], w=["S%d" % u])
                TT("vector", S[:, u * 512:(u + 1) * 512], bkA[:, :], S[:, u * 512:(u + 1) * 512], ALU.add, r=[kA, "S%d" % u], w=["S%d" % u])
                yield

            for d in (0, 1):
                MEMSET("gpsimd", S, 0.0, w=["S0", "S1", "S2", "S3"])
                for a in range(4):
                    CP("vector", negi4[:, a * 128:(a + 1) * 128], cv[:, K_NIF + d, :], r=["consts"], w=["negi4"])
                loads(d, 0)
                for s in range(nsteps):
                    if s + 1 < nsteps:
                        loads(d, s + 1)
                    t0 = seq[d][s]
                    bf = ld[s % 2]
                    kk = "ld%d_" % (s % 2)
                    yb = ybuf[s % 2]
                    yk = "yb%d" % (s % 2)
                    need_out = (t0 < L) or with_ctx_out
                    dt32 = bf["dtla"][:, d * 32:d * 32 + 32]
                    la32 = bf["dtla"][:, 64 + d * 32:64 + d * 32 + 32]
                    TT("gpsimd", v3(xdt, 32), v3(bf["xs"], 32), bc_last(dt32, 64), ALU.mult, r=[kk + "xs", kk + "dtla"], w=["xdt"])
                    pst = PSB[0]
                    MM(pst[:, 0:32], cv[:, K_UF + d, :], la32, True, True, r=[kk + "dtla", "consts"], w=["ps0"])
                    MM(pst[:, 32:64], ones, la32, True, True, r=[kk + "dtla", "consts"], w=["ps0"])
                    MM(pst[:, 64:96], cv[:, K_LF + d, :], la32, True, True, r=[kk + "dtla", "consts"], w=["ps0"])
                    ACT(smt[:, 0:96], pst[:, 0:96], AF.Exp, r=["ps0"], w=["smt"])
                    TT("vector", v3(xdtd, 32), v3(xdt, 32), bc_last(smt[:, 64:96], 64), ALU.mult, r=["xdt", "smt"], w=["xdtd"])
                    gens = [unit(d, u, s, yb, yk) for u in range(4)]
                    alive = True
                    while alive:
                        alive = False
                        for g in gens:
                            try:
                                next(g)
                                alive = True
                            except StopIteration:
                                pass
                    if need_out:
                        dst = (s_yf, s_yb)[d]
                        DMA(rot(["sync", "gpsimd"]), dst[t0:t0 + 128, :], yb, r=[yk + "_%d" % u for u in range(4)])

        def stage_ssd_post(l, with_ctx_out):
            A.reset()
            wbf_flat = A.bf16(16 * 1024)
            wbf = v3(wbf_flat, 16)
            mk = A.off
            stg = [A.f32(8 * 512) for _ in range(2)]
            i = 0
            for half in range(2):
                for cc in range(0, 1024, 512):
                    sk = "wstg%d" % (i % 2)
                    sv = v3(stg[i % 2], 8)
                    DMA(rot(["sync", "gpsimd"]), sv, w_proj_ssm[l, half * 1024:(half + 1) * 1024, cc:cc + 512].rearrange("(kt p) c -> p kt c", p=128), w=[sk])
                    CP(rot(["vector", "gpsimd"]), wbf[:, half * 8:(half + 1) * 8, cc:cc + 512], sv, r=[sk], w=["wbf"])
                    i += 1
            A.off = mk
            p.barrier()
            drow = A.f32(32)
            nwrow = A.f32(2048)
            DMA("sync", drow, ssm_D[l:l + 1, :].partition_broadcast(128), w=["drow"])
            DMA("sync", nwrow, ssm_norm_w[l:l + 1, :].partition_broadcast(128), w=["nwrow"])
            bufs = [dict(yf=A.f32(2048), yb=A.f32(2048), xs=A.f32(2048), z=A.f32(2048), gb=A.f32(1024)) for _ in range(2)]
            sq = A.f32(2048)
            ss = A.f32(8)
            yT = [A.bf16(16 * 128) for _ in range(2)]
            pbt = [A.f32(1024) for _ in range(2)]
            chunks = [c * 128 for c in range(GW)] + ([L, L + 128] if with_ctx_out else [])
            colv = s_pb[0:L, :].rearrange("(r c) d -> c r d", c=GW)

            def loads(i):
                t0 = chunks[i]
                bf = bufs[i % 2]
                kk = "pl%d_" % (i % 2)
                DMA("sync", bf["yf"], s_yf[t0:t0 + 128, :], w=[kk + "yf"])
                DMA("gpsimd", bf["yb"], s_yb[t0:t0 + 128, :], w=[kk + "yb"])
                DMA("sync", bf["xs"], s_xs[t0:t0 + 128, :], w=[kk + "xs"])
                DMA("gpsimd", bf["z"], s_z[t0:t0 + 128, :], w=[kk + "z"])
                DMA("sync", bf["gb"], s_gbt[t0:t0 + 128, :], w=[kk + "gb"])

            loads(0)
            for i, t0 in enumerate(chunks):
                if i + 1 < len(chunks):
                    loads(i + 1)
                bf = bufs[i % 2]
                kk = "pl%d_" % (i % 2)
                y = bf["yf"]
                yk = kk + "yf"
                TT("gpsimd", y, y, bf["yb"], ALU.add, r=[yk, kk + "yb"], w=[yk])
                TT("vector", v3(bf["xs"], 32), v3(bf["xs"], 32), bc_last(drow, 64), ALU.mult, r=[kk + "xs", "drow"], w=[kk + "xs"])
                TT("gpsimd", y, y, bf["xs"], ALU.add, r=[yk, kk + "xs"], w=[yk])
                TT("vector", y, y, bf["z"], ALU.mult, r=[yk, kk + "z"], w=[yk])
                TT("gpsimd", sq, y, y, ALU.mult, r=[yk], w=["sq"])
                p.op("vector", lambda en: en.tensor_reduce(ss, v3(sq, 8), AX.X, ALU.add), r=["sq"], w=["ss"])
                ACT(ss, ss, AF.Sqrt, r=["ss", "epsc"], w=["ss"], bias=epsc[:, 0:1], scale=1.0 / 256)
                RECIP(ss, ss, r=["ss"], w=["ss"])
                TT("vector", v3(y, 8), v3(y, 8), bc_last(ss, 256), ALU.mult, r=[yk, "ss"], w=[yk])
                TT("gpsimd", y, y, nwrow, ALU.mult, r=[yk, "nwrow"], w=[yk])
                yt = yT[i % 2]
                ytk = "yT%d" % (i % 2)
                yt3 = v3(yt, 16)
                for q4 in range(4):
                    pst = PSB[q4 % 2]
                    pk = "ps%d" % (q4 % 2)
                    for a in range(4):
                        kt = q4 * 4 + a
                        TRP(pst[:, a * 128:(a + 1) * 128], y[:, kt * 128:(kt + 1) * 128], r=[yk], w=[pk])
                    CP(rot(["vector", "scalar"]), yt[:, q4 * 512:(q4 + 1) * 512], pst[:, :], r=[pk], w=[ytk])
                pb = pbt[i % 2]
                pbk = "pbt%d" % (i % 2)
                for half in range(2):
                    pst = PSB[2 + half]
                    pk = "ps%d" % (2 + half)
                    for kt in range(16):
                        MM(pst[:, :], yt3[:, kt, :], wbf[:, kt, half * 512:(half + 1) * 512], kt == 0, kt == 15, r=[ytk, "wbf"], w=[pk])
                    TT("vector", pb[:, half * 512:(half + 1) * 512], pst[:, :], bf["gb"][:, half * 512:(half + 1) * 512], ALU.mult, r=[pk, kk + "gb"], w=[pbk])
                dst = colv[t0 // 128] if t0 < L else s_pb[t0:t0 + 128, :]
                DMA(rot(["sync", "gpsimd"]), dst, pb, r=[pbk])


        def load_w_bf16(src, K, N, wbf, key):
            mk = A.off
            stg = [A.f32(8 * 512) for _ in range(2)]
            i = 0
            for k0 in range(0, K // 128, 8):
                for cc in range(0, N, 512):
                    sk = "wstg%d" % (i % 2)
                    sv = v3(stg[i % 2], 8)
                    DMA(rot(["sync", "gpsimd"]), sv, src[k0 * 128:(k0 + 8) * 128, cc:cc + 512].rearrange("(kt p) c -> p kt c", p=128), w=[sk])
                    CP(rot(["vector", "gpsimd", "scalar"]), wbf[:, k0:k0 + 8, cc:cc + 512], sv, r=[sk], w=[key])
                    i += 1
            p.barrier()
            A.off = mk

        def layer_norm_rows(r, rk, grow_, brow_, st6, mv):
            p.op("vector", lambda en: en.bn_stats(st6[:, 0:6], r[:, 0:512]), r=[rk], w=["lnst"])
            p.op("vector", lambda en: en.bn_stats(st6[:, 6:12], r[:, 512:1024]), r=[rk], w=["lnst"])
            p.op("vector", lambda en: en.bn_aggr(mv, v3(st6, 2)), r=["lnst"], w=["lnmv"])
            ACT(mv[:, 1:2], mv[:, 1:2], AF.Sqrt, r=["lnmv", "epsc"], w=["lnmv"], bias=epsc[:, 1:2], scale=1.0)
            RECIP(mv[:, 1:2], mv[:, 1:2], r=["lnmv"], w=["lnmv"])
            TS("vector", r, r, mv[:, 0:1], mv[:, 1:2], ALU.subtract, ALU.mult, r=[rk, "lnmv"], w=[rk])
            TT("gpsimd", r, r, grow_, ALU.mult, r=[rk, "lnrows"], w=[rk])
            TT("gpsimd", r, r, brow_, ALU.add, r=[rk, "lnrows"], w=[rk])

        def xsrc(l, t0):
            if l == 0:
                return x_in[t0:t0 + 128, :] if t0 < L else ctx_in[t0 - L:t0 - L + 128, :]
            return xprev[t0:t0 + 128, :]

        def stage3a(l, with_ctx):
            A.reset()
            wpg = v3(A.bf16(8 * 1024), 8)
            wo = v3(A.bf16(8 * 1024), 8)
            load_w_bf16(w_proj_gdn[l], 1024, 1024, wpg, "wpg")
            load_w_bf16(w_out[l], 1024, 1024, wo, "wo")
            gnw = A.f32(128)
            g1r = A.f32(2048)
            lng = A.f32(1024)
            lnb = A.f32(1024)
            DMA("sync", gnw, gdn_norm_w[l:l + 1, :].partition_broadcast(128), w=["rows"])
            DMA("sync", g1r, s_grow[:, 0:2048], w=["rows"])
            DMA("sync", lng, ln1_g[l:l + 1, :].partition_broadcast(128), w=["rows"])
            DMA("sync", lnb, ln1_b[l:l + 1, :].partition_broadcast(128), w=["rows"])
            p.barrier()
            bufs = [dict(of=A.f32(1024), ob=A.f32(1024), go=A.f32(1024), ga=A.f32(1024), pb=A.f32(1024), x=A.f32(1024)) for _ in range(2)]
            sq = A.f32(1024)
            ss = A.f32(8)
            st6 = A.f32(12)
            mv = A.f32(2)
            yaT = [A.bf16(1024) for _ in range(2)]
            mT = [A.bf16(1024) for _ in range(2)]
            tiles = [c * 128 for c in range(L // 128)] + ([L, L + 128] if with_ctx else [])

            def loads(i):
                t0 = tiles[i]
                bf = bufs[i % 2]
                kk = "a%d_" % (i % 2)
                DMA("sync", bf["of"], s_of[t0:t0 + 128, :], w=[kk + "of"])
                DMA("gpsimd", bf["ob"], s_ob[t0:t0 + 128, :], w=[kk + "ob"])
                DMA("sync", bf["go"], s_go[t0:t0 + 128, :], w=[kk + "go"])
                DMA("gpsimd", bf["ga"], s_ga[t0:t0 + 128, :], w=[kk + "ga"])
                DMA("sync", bf["pb"], s_pb[t0:t0 + 128, :], w=[kk + "pb"])
                DMA("gpsimd", bf["x"], xsrc(l, t0), w=[kk + "x"])

            loads(0)
            for i, t0 in enumerate(tiles):
                if i + 1 < len(tiles):
                    loads(i + 1)
                bf = bufs[i % 2]
                kk = "a%d_" % (i % 2)
                o = bf["of"]
                ok = kk + "of"
                mi = 1 if t0 >= L else 0
                TT("gpsimd", o, o, bf["ob"], ALU.add, r=[ok, kk + "ob"], w=[ok])
                TT("gpsimd", sq, o, o, ALU.mult, r=[ok], w=["sq"])
                p.op("vector", lambda en: en.tensor_reduce(ss, v3(sq, 8), AX.X, ALU.add), r=["sq"], w=["ss"])
                ACT(ss, ss, AF.Sqrt, r=["ss", "epsc"], w=["ss"], bias=epsc[:, 0:1], scale=1.0 / 128)
                RECIP(ss, ss, r=["ss"], w=["ss"])
                TT("vector", v3(o, 8), v3(o, 8), bc_last(ss, 128), ALU.mult, r=[ok, "ss"], w=[ok])
                TT("gpsimd", v3(o, 8), v3(o, 8), bc_mid(gnw, 8), ALU.mult, r=[ok, "rows"], w=[ok])
                TT("vector", o, o, bf["go"], ALU.mult, r=[ok, kk + "go"], w=[ok])
                yt = yaT[i % 2]
                ytk = "yaT%d" % (i % 2)
                for q4 in range(2):
                    pst = PSB[q4]
                    pk = "ps%d" % q4
                    for a in range(4):
                        kt = q4 * 4 + a
                        TRP(pst[:, a * 128:(a + 1) * 128], o[:, kt * 128:(kt + 1) * 128], r=[ok], w=[pk])
                    CP(("vector", "scalar")[q4], yt[:, q4 * 512:(q4 + 1) * 512], pst[:, :], r=[pk], w=[ytk])
                yt3 = v3(yt, 8)
                ms = bf["ga"]
                msk = kk + "ga"
                for half in range(2):
                    pst = PSB[2 + half]
                    pk = "ps%d" % (2 + half)
                    for kt in range(8):
                        MM(pst[:, :], yt3[:, kt, :], wpg[:, kt, half * 512:(half + 1) * 512], kt == 0, kt == 7, r=[ytk, "wpg"], w=[pk])
                    TT("vector", ms[:, half * 512:(half + 1) * 512], pst[:, :], ms[:, half * 512:(half + 1) * 512], ALU.mult, r=[pk, msk], w=[msk])
                TT("gpsimd", ms, ms, bf["pb"], ALU.add, r=[msk, kk + "pb"], w=[msk])
                mt = mT[i % 2]
                mtk = "mT%d" % (i % 2)
                for q4 in range(2):
                    pst = PSB[4 + q4]
                    pk = "ps%d" % (4 + q4)
                    for a in range(4):
                        kt = q4 * 4 + a
                        TRP(pst[:, a * 128:(a + 1) * 128], ms[:, kt * 128:(kt + 1) * 128], r=[msk], w=[pk])
                    CP(("vector", "scalar")[q4], mt[:, q4 * 512:(q4 + 1) * 512], pst[:, :], r=[pk], w=[mtk])
                mt3 = v3(mt, 8)
                r_ = bf["go"]
                rk = kk + "go"
                for half in range(2):
                    pst = PSB[6 + half]
                    pk = "ps%d" % (6 + half)
                    for kt in range(8):
                        MM(pst[:, :], mt3[:, kt, :], wo[:, kt, half * 512:(half + 1) * 512], kt == 0, kt == 7, r=[mtk, "wo"], w=[pk])
                    TT("vector", r_[:, half * 512:(half + 1) * 512], pst[:, :], g1r[:, mi * 1024 + half * 512:mi * 1024 + (half + 1) * 512], ALU.mult,
                       r=[pk, "rows"], w=[rk])
                STT(r_, bf["x"], ALPHA, r_, ALU.mult, ALU.add, r=[kk + "x", rk], w=[rk])
                layer_norm_rows(r_, rk, lng, lnb, st6, mv)
                DMA(rot(["sync", "gpsimd"]), s_x1[t0:t0 + 128, :], r_, r=[rk])

        def stage3b(l, with_ctx, final):
            A.reset()
            w1 = v3(A.bf16(8 * 4096), 8)
            w2 = v3(A.bf16(32 * 1024), 32)
            load_w_bf16(w_ff1[l], 1024, 4096, w1, "w1")
            load_w_bf16(w_ff2[l], 4096, 1024, w2, "w2")
            g2r = A.f32(2048)
            lng = A.f32(1024)
            lnb = A.f32(1024)
            b2r = A.f32(1024)
            DMA("sync", g2r, s_grow[:, 2048:4096], w=["rows"])
            DMA("sync", lng, ln2_g[l:l + 1, :].partition_broadcast(128), w=["rows"])
            DMA("sync", lnb, ln2_b[l:l + 1, :].partition_broadcast(128), w=["rows"])
            DMA("sync", b2r, b_ff2[l:l + 1, :].partition_broadcast(128), w=["rows"])
            p.barrier()
            xb = [A.f32(1024) for _ in range(2)]
            rb = [A.f32(1024)] * 2
            h2T = [A.bf16(1024) for _ in range(2)]
            hid = [A.bf16(32 * 128)] * 2
            rl = [A.f32(512) for _ in range(2)]
            st6 = A.f32(12)
            mv = A.f32(2)
            tiles = [c * 128 for c in range(L // 128)] + ([L, L + 128] if with_ctx else [])

            def loads(i):
                DMA(rot(["sync", "gpsimd"]), xb[i % 2], s_x1[tiles[i]:tiles[i] + 128, :], w=["xb%d" % (i % 2)])

            loads(0)
            for i, t0 in enumerate(tiles):
                if i + 1 < len(tiles):
                    loads(i + 1)
                x1 = xb[i % 2]
                xk = "xb%d" % (i % 2)
                mi = 1 if t0 >= L else 0
                ht = h2T[i % 2]
                hk = "h2T%d" % (i % 2)
                ht3 = v3(ht, 8)
                for q4 in range(2):
                    pst = PSB[q4]
                    pk = "ps%d" % q4
                    for a in range(4):
                        kt = q4 * 4 + a
                        TRP(pst[:, a * 128:(a + 1) * 128], x1[:, kt * 128:(kt + 1) * 128], r=[xk], w=[pk])
                    for a in range(4):
                        kt = q4 * 4 + a
                        if a % 2 == 0:
                            ACT(ht3[:, kt, :], pst[:, a * 128:(a + 1) * 128], AF.Identity, r=[pk, "modcol"], w=[hk],
                                bias=modcv[:, 24 + kt, mi:mi + 1], scale=modcv[:, 32 + kt, mi:mi + 1])
                        else:
                            TS("vector", ht3[:, kt, :], pst[:, a * 128:(a + 1) * 128], modcv[:, 32 + kt, mi:mi + 1], modcv[:, 24 + kt, mi:mi + 1],
                               ALU.mult, ALU.add, r=[pk, "modcol"], w=[hk])
                hd = hid[i % 2]
                hdk = "hid"
                for q in range(8):
                    pst = PSB[2 + q % 2]
                    pk = "ps%d" % (2 + q % 2)
                    rt = rl[q % 2]
                    rtk = "rl%d" % (q % 2)
                    for a in range(4):
                        hf = q * 4 + a
                        for kt in range(8):
                            MM(pst[:, a * 128:(a + 1) * 128], w1[:, kt, hf * 128:(hf + 1) * 128], ht3[:, kt, :], kt == 0, kt == 7, r=[hk, "w1"], w=[pk])
                    for a in range(4):
                        hf = q * 4 + a
                        ACT(rt[:, a * 128:(a + 1) * 128], pst[:, a * 128:(a + 1) * 128], AF.Relu, r=[pk, "bf1"], w=[rtk], bias=bf1v[:, l, hf:hf + 1])
                    TT("gpsimd", hd[:, q * 512:(q + 1) * 512], rt, rt, ALU.mult, r=[rtk], w=[hdk])
                hd3 = v3(hd, 32)
                r_ = rb[i % 2]
                rk = "rb"
                for half in range(2):
                    pst = PSB[4 + half]
                    pk = "ps%d" % (4 + half)
                    for hf in range(32):
                        MM(pst[:, :], hd3[:, hf, :], w2[:, hf, half * 512:(half + 1) * 512], hf == 0, hf == 31, r=[hdk, "w2"], w=[pk])
                    TT("vector", r_[:, half * 512:(half + 1) * 512], pst[:, :], b2r[:, half * 512:(half + 1) * 512], ALU.add, r=[pk, "rows"], w=[rk])
                TT("gpsimd", r_, r_, g2r[:, mi * 1024:(mi + 1) * 1024], ALU.mult, r=[rk, "rows"], w=[rk])
                STT(r_, x1, ALPHA, r_, ALU.mult, ALU.add, r=[xk, rk], w=[rk])
                layer_norm_rows(r_, rk, lng, lnb, st6, mv)
                dst = y_out[t0:t0 + 128, :] if final else s_x2[t0:t0 + 128, :]
                DMA(rot(["sync", "gpsimd"]), dst, r_, r=[rk])

        xprev = s_x2
        for l in range(DEPTH):
            last = l == DEPTH - 1
            stage0(l)
            p.barrier()
            if stop_after == "s0":
                break
            stage1(l)
            if stop_after == "s1":
                break
            stage_gdn(l, (not last) or force_ctx_out)
            p.barrier()
            if stop_after == "gdn":
                break
            stage_ssd(l, (not last) or force_ctx_out)
            p.barrier()
            stage_ssd_post(l, (not last) or force_ctx_out)
            p.barrier()
            if stop_after == "ssd":
                break
            stage3a(l, (not last) or force_ctx_out)
            p.barrier()
            if stop_after == "s3a":
                break
            stage3b(l, (not last) or force_ctx_out, last and not force_ctx_out)
            p.barrier()
            if stop_after == "s3b":
                break

        if "d_modcol" in dbg:
            dd = nc.dram_tensor("d_modcol", [128, 96], F32, kind="ExternalOutput").ap()
            DMA("sync", dd, modcol, r=["modcol"])
        p.barrier()
        p.finish()
    return nc


def make_in_maps(inputs, GW=64, DEPTH=2, ncores=8):
    f = lambda a: np.ascontiguousarray(np.asarray(a, dtype=np.float32))
    shared = {
        "consts": make_consts(),
        "w_mod": f(inputs["w_mod"]), "b_mod": f(inputs["b_mod"]), "w_in": f(inputs["w_in"]),
        "gcw_col": f(np.asarray(inputs["gdn_conv_w"]).reshape(DEPTH, 5, 24, 128).transpose(3, 0, 2, 1)),
        "scw_col": f(np.asarray(inputs["ssm_conv_w"]).reshape(DEPTH, 5, 32, 128).transpose(3, 0, 2, 1)),
        "scb_col": f(np.asarray(inputs["ssm_conv_b"]).reshape(DEPTH, 32, 128).transpose(2, 0, 1)),
        "gdn_A_log": f(np.asarray(inputs["gdn_A_log"]).reshape(DEPTH, 16)),
        "gdn_dt_bias": f(np.asarray(inputs["gdn_dt_bias"]).reshape(DEPTH, 16)),
        "gdn_norm_w": f(inputs["gdn_norm_w"]),
        "ssm_A_log": f(np.asarray(inputs["ssm_A_log"]).reshape(DEPTH, 64)),
        "ssm_dt_bias": f(np.asarray(inputs["ssm_dt_bias"]).reshape(DEPTH, 64)),
        "ssm_D": f(inputs["ssm_D"]), "ssm_norm_w": f(inputs["ssm_norm_w"]),
        "w_proj_gdn": f(inputs["w_proj_gdn"]), "w_proj_ssm": f(inputs["w_proj_ssm"]), "w_out": f(inputs["w_out"]),
        "ln1_g": f(inputs["ln1_g"]), "ln1_b": f(inputs["ln1_b"]),
        "w_ff1": f(inputs["w_ff1"]),
        "bff1_col": f(np.asarray(inputs["b_ff1"]).reshape(DEPTH, 32, 128).transpose(2, 0, 1)),
        "w_ff2": f(inputs["w_ff2"]), "b_ff2": f(inputs["b_ff2"]),
        "ln2_g": f(inputs["ln2_g"]), "ln2_b": f(inputs["ln2_b"]),
    }
    x = np.asarray(inputs["x"]); c = np.asarray(inputs["c"]); ctx = np.asarray(inputs["ctx"]); c_ctx = np.asarray(inputs["c_ctx"])
    maps = []
    for b in range(ncores):
        bb = b % x.shape[0]
        m = dict(shared)
        m["x"] = f(x[bb])
        m["ctx"] = f(ctx[bb])
        m["c_col"] = f(np.stack([c[bb], c_ctx]).reshape(2, 8, 128).transpose(2, 1, 0))
        maps.append(m)
    return maps


_NC_CACHE = {}


def kernel(**inputs):
    key = (64, 2)
    if key not in _NC_CACHE:
        _NC_CACHE[key] = build_nc(64, 2)
    nc = _NC_CACHE[key]
    maps = make_in_maps(inputs, 64, 2, 8)
    res = run_bass_kernel_spmd(nc, maps, core_ids=list(range(8)))
    return np.stack([np.asarray(r["y"], dtype=np.float32) for r in res.results], axis=0)
```
